# Optimizing a Trainium2 kernel written in Bass

```python
import jax, jax.numpy as jnp
from jax import lax
import numpy as np

D_MODEL = 1024
BATCH = 2
SEQ = 16384
DEPTH = 4
DEC_BATCH = 8
DEC_SEQ = 8192
PAST_LEN = 128

HEAD_DIM = 64
A_HEADS = 8
A_KV_HEADS = 2
WINDOW = 128
BLOCK = 128
B_HEADS = 8
B_KV_HEADS = 2
ROPE_BASE = 10000.0
GRID_W = 64
C_WIDTH = 1024
C_BLOCKS = 8
C_BLOCK_W = C_WIDTH // C_BLOCKS
C_CONV = 4
LRU_C = 8.0
N_BRANCH = 3
X_HEADS = 4
X_HEAD_DIM = 128
MEM_LEN = 256
D_FF = 3 * D_MODEL
FFN_CONV = 3
ALPHA = (2.0 * DEPTH) ** 0.25
BETA = (8.0 * DEPTH) ** -0.25
LN_EPS = 1e-5
RMS_EPS = 1e-6
NEG_INF = -1e30

A_Q = A_HEADS * HEAD_DIM
A_KV = A_KV_HEADS * HEAD_DIM
B_Q = B_HEADS * HEAD_DIM
B_KV = B_KV_HEADS * HEAD_DIM
X_W = X_HEADS * X_HEAD_DIM
SPLIT_SIZES = (A_Q, A_KV, A_KV, B_Q, B_KV, B_KV, C_WIDTH, C_WIDTH, N_BRANCH * D_MODEL)
N_IN = sum(SPLIT_SIZES)
SPLIT_POINTS = tuple(int(v) for v in np.cumsum(SPLIT_SIZES)[:-1])

kernel_name = 'hybrid_bidir_encoder'


def layer_norm(x, g, b):
    xf = x.astype(jnp.float32)
    mu = jnp.mean(xf, axis=-1, keepdims=True)
    var = jnp.mean(jnp.square(xf - mu), axis=-1, keepdims=True)
    y = (xf - mu) * lax.rsqrt(var + LN_EPS) * g.astype(jnp.float32) + b.astype(jnp.float32)
    return y.astype(x.dtype)


def rms_norm(x, g):
    xf = x.astype(jnp.float32)
    y = xf * lax.rsqrt(jnp.mean(jnp.square(xf), axis=-1, keepdims=True) + RMS_EPS) * g.astype(jnp.float32)
    return y.astype(x.dtype)


def depthwise_conv(x, w, b):
    K = w.shape[0]
    T = x.shape[1]
    left = (K - 1) // 2
    xp = jnp.pad(x, ((0, 0), (left, K - 1 - left), (0, 0)))
    out = b
    for k in range(K):
        out = out + xp[:, k:k + T] * w[k]
    return out


def alibi_slopes(n_heads):
    return jnp.asarray([2.0 ** (-8.0 * (h + 1) / n_heads) for h in range(n_heads)], dtype=jnp.float32)


def window_attention(q, k, v, sink, slopes):
    Bsz, T, H, hd = q.shape
    KV = k.shape[2]
    G = H // KV
    nb = T // BLOCK
    qb = q.reshape(Bsz, nb, BLOCK, KV, G, hd)

    def neighbours(z):
        zp = jnp.pad(z, ((0, 0), (BLOCK, BLOCK), (0, 0), (0, 0)))
        zb = zp.reshape(Bsz, nb + 2, BLOCK, KV, hd)
        return jnp.concatenate([zb[:, :-2], zb[:, 1:-1], zb[:, 2:]], axis=2)

    kw = neighbours(k)
    vw = neighbours(v)
    s = jnp.einsum('bnqkgd,bnskd->bnkgqs', qb, kw).astype(jnp.float32) * (hd ** -0.5)
    blk = jnp.arange(nb)
    qpos = blk[:, None] * BLOCK + jnp.arange(BLOCK)[None, :]
    kpos = (blk[:, None] - 1) * BLOCK + jnp.arange(3 * BLOCK)[None, :]
    dist = jnp.abs(qpos[:, :, None] - kpos[:, None, :])
    valid = (dist <= WINDOW) & (kpos[:, None, :] >= 0) & (kpos[:, None, :] < T)
    bias = -slopes.reshape(KV, G)[None, None, :, :, None, None] * dist.astype(jnp.float32)[None, :, None, None, :, :]
    s = jnp.where(valid[None, :, None, None], s + bias, NEG_INF)
    sink_l = sink.astype(jnp.float32).reshape(KV, G)[None, None, :, :, None, None]
    m = jnp.maximum(jnp.max(s, axis=-1, keepdims=True), sink_l)
    p = jnp.exp(s - m)
    den = jnp.sum(p, axis=-1, keepdims=True) + jnp.exp(sink_l - m)
    p = (p / den).astype(v.dtype)
    o = jnp.einsum('bnkgqs,bnskd->bnqkgd', p, vw)
    return o.reshape(Bsz, T, H * hd)


def axial_rope_angles(T):
    rows = T // GRID_W
    row = jnp.repeat(jnp.arange(rows, dtype=jnp.float32), GRID_W)
    col = jnp.tile(jnp.arange(GRID_W, dtype=jnp.float32), rows)
    axis_dim = HEAD_DIM // 2
    inv_freq = ROPE_BASE ** (-jnp.arange(0, axis_dim, 2, dtype=jnp.float32) / axis_dim)
    ang = jnp.concatenate([row[:, None] * inv_freq, col[:, None] * inv_freq], axis=-1)
    return jnp.cos(ang), jnp.sin(ang)


def apply_rope(x, cos, sin):
    xf = x.astype(jnp.float32).reshape(x.shape[:-1] + (x.shape[-1] // 2, 2))
    x0 = xf[..., 0]
    x1 = xf[..., 1]
    c = cos[None, :, None, :]
    s = sin[None, :, None, :]
    out = jnp.stack([x0 * c - x1 * s, x0 * s + x1 * c], axis=-1).reshape(x.shape)
    return out.astype(x.dtype)


def dense_block_attention(q, k, v):
    Bsz, T, H, hd = q.shape
    KV = k.shape[2]
    G = H // KV
    nb = T // BLOCK
    qb = q.reshape(Bsz, nb, BLOCK, KV, G, hd).transpose(1, 0, 2, 3, 4, 5)

    def one_block(q_blk):
        s = jnp.einsum('bqkgd,bskd->bkgqs', q_blk, k).astype(jnp.float32) * (hd ** -0.5)
        p = jax.nn.softmax(s, axis=-1).astype(v.dtype)
        return jnp.einsum('bkgqs,bskd->bqkgd', p, v)

    o = lax.map(one_block, qb)
    return o.transpose(1, 0, 2, 3, 4, 5).reshape(Bsz, T, H * hd)


def _lin_rec_combine(e1, e2):
    a1, b1 = e1
    a2, b2 = e2
    return a1 * a2, a2 * b1 + b2


def rg_lru(u, w_r, b_r, w_i, b_i, lam):
    Bsz, T, C = u.shape
    ub = u.reshape(Bsz, T, C_BLOCKS, C_BLOCK_W)
    r = jax.nn.sigmoid(jnp.einsum('btnc,ncd->btnd', ub, w_r.astype(jnp.float32)).reshape(Bsz, T, C) + b_r)
    i = jax.nn.sigmoid(jnp.einsum('btnc,ncd->btnd', ub, w_i.astype(jnp.float32)).reshape(Bsz, T, C) + b_i)
    log_a = -LRU_C * r * jax.nn.softplus(-lam.astype(jnp.float32))
    a = jnp.exp(log_a)
    mult = jnp.sqrt(-jnp.expm1(2.0 * log_a))
    _, h = lax.associative_scan(_lin_rec_combine, (a, mult * i * u), axis=1)
    return h


def token_mixer(x, w_in, sink_a, q_norm_b, k_norm_b, conv_c_w, conv_c_b, w_rec_gate, b_rec_gate,
                w_in_gate, b_in_gate, lru_lambda, w_br_a, w_br_b, w_br_c, w_out):
    Bsz, T, _ = x.shape
    z = x @ w_in
    qa, ka, va, qb, kb, vb, xc, gc, zg = jnp.split(z, SPLIT_POINTS, axis=-1)
    oa = window_attention(qa.reshape(Bsz, T, A_HEADS, HEAD_DIM),
                          ka.reshape(Bsz, T, A_KV_HEADS, HEAD_DIM),
                          va.reshape(Bsz, T, A_KV_HEADS, HEAD_DIM),
                          sink_a, alibi_slopes(A_HEADS))
    cos, sin = axial_rope_angles(T)
    qb = apply_rope(rms_norm(qb.reshape(Bsz, T, B_HEADS, HEAD_DIM), q_norm_b), cos, sin)
    kb = apply_rope(rms_norm(kb.reshape(Bsz, T, B_KV_HEADS, HEAD_DIM), k_norm_b), cos, sin)
    ob = dense_block_attention(qb, kb, vb.reshape(Bsz, T, B_KV_HEADS, HEAD_DIM))
    u = depthwise_conv(xc, conv_c_w, conv_c_b).astype(jnp.float32)
    h_fwd = rg_lru(u, w_rec_gate[0], b_rec_gate[0], w_in_gate[0], b_in_gate[0], lru_lambda[0])
    h_bwd = rg_lru(u[:, ::-1], w_rec_gate[1], b_rec_gate[1], w_in_gate[1], b_in_gate[1], lru_lambda[1])[:, ::-1]
    oc = (h_fwd + h_bwd).astype(x.dtype) * jax.nn.gelu(gc)
    g = jax.nn.sigmoid(zg.reshape(Bsz, T, N_BRANCH, D_MODEL).astype(jnp.float32)).astype(x.dtype)
    merged = g[:, :, 0] * (oa @ w_br_a) + g[:, :, 1] * (ob @ w_br_b) + g[:, :, 2] * (oc @ w_br_c)
    return merged @ w_out


def memory_cross_attention(x, mem, w_cq, w_ckv, w_co):
    Bsz, T, _ = x.shape
    q = (x @ w_cq).reshape(Bsz, T, X_HEADS, X_HEAD_DIM)
    kv = (mem @ w_ckv).reshape(Bsz, mem.shape[1], 2, X_HEADS, X_HEAD_DIM)
    k = kv[:, :, 0]
    v = kv[:, :, 1]
    s = jnp.einsum('bthd,bmhd->bhtm', q, k).astype(jnp.float32) * (X_HEAD_DIM ** -0.5)
    p = jax.nn.softmax(s, axis=-1).astype(v.dtype)
    o = jnp.einsum('bhtm,bmhd->bthd', p, v).reshape(Bsz, T, X_W)
    return o @ w_co


def conv_glu_ffn(x, w_up, conv_f_w, conv_f_b, w_down):
    gate, up = jnp.split(x @ w_up, 2, axis=-1)
    gate = depthwise_conv(gate, conv_f_w, conv_f_b)
    return (jax.nn.gelu(gate) * up) @ w_down


def encoder_layer(x, mem, w_in, sink_a, q_norm_b, k_norm_b, conv_c_w, conv_c_b, w_rec_gate, b_rec_gate,
                  w_in_gate, b_in_gate, lru_lambda, w_br_a, w_br_b, w_br_c, w_out, ln1_g, ln1_b,
                  w_cq, w_ckv, w_co, ln2_g, ln2_b, w_up, conv_f_w, conv_f_b, w_down, ln3_g, ln3_b):
    mix = token_mixer(x, w_in, sink_a, q_norm_b, k_norm_b, conv_c_w, conv_c_b, w_rec_gate, b_rec_gate,
                      w_in_gate, b_in_gate, lru_lambda, w_br_a, w_br_b, w_br_c, w_out)
    x = layer_norm(ALPHA * x + mix, ln1_g, ln1_b)
    x = layer_norm(ALPHA * x + memory_cross_attention(x, mem, w_cq, w_ckv, w_co), ln2_g, ln2_b)
    x = layer_norm(ALPHA * x + conv_glu_ffn(x, w_up, conv_f_w, conv_f_b, w_down), ln3_g, ln3_b)
    return x


def setup_inputs(seed: int = 0) -> dict:
    key = jax.random.key(seed)
    ks = iter(jax.random.split(key, 40))

    def nrm(shape, scale):
        return jax.random.normal(next(ks), shape, jnp.float32) * scale

    L = DEPTH
    a_c = jax.random.uniform(next(ks), (L, 2, C_WIDTH), jnp.float32, 0.81, 0.998)
    s_lam = a_c ** (1.0 / LRU_C)
    return {
        'x_prompt': nrm((BATCH, SEQ, D_MODEL), 1.0),
        'x_sample': nrm((DEC_BATCH, DEC_SEQ, D_MODEL), 1.0),
        'mem_prompt': nrm((BATCH, MEM_LEN, D_MODEL), 1.0),
        'mem_sample': nrm((DEC_BATCH, MEM_LEN, D_MODEL), 1.0),
        'w_in': nrm((L, D_MODEL, N_IN), D_MODEL ** -0.5),
        'sink_a': nrm((L, A_HEADS), 0.5),
        'q_norm_b': 1.0 + nrm((L, HEAD_DIM), 0.1),
        'k_norm_b': 1.0 + nrm((L, HEAD_DIM), 0.1),
        'conv_c_w': nrm((L, C_CONV, C_WIDTH), C_CONV ** -0.5),
        'conv_c_b': nrm((L, C_WIDTH), 0.01),
        'w_rec_gate': nrm((L, 2, C_BLOCKS, C_BLOCK_W, C_BLOCK_W), C_BLOCK_W ** -0.5),
        'b_rec_gate': nrm((L, 2, C_WIDTH), 0.01),
        'w_in_gate': nrm((L, 2, C_BLOCKS, C_BLOCK_W, C_BLOCK_W), C_BLOCK_W ** -0.5),
        'b_in_gate': nrm((L, 2, C_WIDTH), 0.01),
        'lru_lambda': jnp.log(s_lam) - jnp.log1p(-s_lam),
        'w_br_a': nrm((L, A_Q, D_MODEL), A_Q ** -0.5),
        'w_br_b': nrm((L, B_Q, D_MODEL), B_Q ** -0.5),
        'w_br_c': nrm((L, C_WIDTH, D_MODEL), C_WIDTH ** -0.5),
        'w_out': nrm((L, D_MODEL, D_MODEL), BETA * D_MODEL ** -0.5),
        'ln1_g': 1.0 + nrm((L, D_MODEL), 0.05),
        'ln1_b': nrm((L, D_MODEL), 0.02),
        'w_cq': nrm((L, D_MODEL, X_W), D_MODEL ** -0.5),
        'w_ckv': nrm((L, D_MODEL, 2 * X_W), D_MODEL ** -0.5),
        'w_co': nrm((L, X_W, D_MODEL), BETA * X_W ** -0.5),
        'ln2_g': 1.0 + nrm((L, D_MODEL), 0.05),
        'ln2_b': nrm((L, D_MODEL), 0.02),
        'w_up': nrm((L, D_MODEL, 2 * D_FF), D_MODEL ** -0.5),
        'conv_f_w': nrm((L, FFN_CONV, D_FF), FFN_CONV ** -0.5),
        'conv_f_b': nrm((L, D_FF), 0.01),
        'w_down': nrm((L, D_FF, D_MODEL), BETA * D_FF ** -0.5),
        'ln3_g': 1.0 + nrm((L, D_MODEL), 0.05),
        'ln3_b': nrm((L, D_MODEL), 0.02),
    }


def reference(x_prompt, x_sample, mem_prompt, mem_sample, w_in, sink_a, q_norm_b, k_norm_b, conv_c_w, conv_c_b,
              w_rec_gate, b_rec_gate, w_in_gate, b_in_gate, lru_lambda, w_br_a, w_br_b, w_br_c, w_out,
              ln1_g, ln1_b, w_cq, w_ckv, w_co, ln2_g, ln2_b, w_up, conv_f_w, conv_f_b, w_down, ln3_g, ln3_b):
    weights = (w_in, sink_a, q_norm_b, k_norm_b, conv_c_w, conv_c_b, w_rec_gate, b_rec_gate, w_in_gate, b_in_gate,
               lru_lambda, w_br_a, w_br_b, w_br_c, w_out, ln1_g, ln1_b, w_cq, w_ckv, w_co, ln2_g, ln2_b,
               w_up, conv_f_w, conv_f_b, w_down, ln3_g, ln3_b)
    y_prompt = x_prompt
    y_sample = x_sample
    for layer in range(DEPTH):
        wl = [w[layer] for w in weights]
        y_prompt = encoder_layer(y_prompt, mem_prompt, *wl)
        y_sample = encoder_layer(y_sample, mem_sample, *wl)
    return (y_prompt, y_sample)
```

```python
import math
from contextlib import ExitStack
import numpy as np
import concourse.bass as bass
import concourse.mybir as mybir
from concourse.bass_utils import run_bass_kernel_spmd

F32 = mybir.dt.float32
BF16 = mybir.dt.bfloat16
AF = mybir.ActivationFunctionType
ALU = mybir.AluOpType

D = 1024
NIN = 6656
DFF = 3072
MEM = 256
ALPHA = (2.0 * 4) ** 0.25
LN_EPS = 1e-5
RMS_EPS = 1e-6
PO = 512
NEG = -30000.0
NPV = 200


class K:
    ENG = ('pe', 'act', 'dve', 'pool', 'sp')
    GRP = ('l0', 'l1', 'st', 'w')

    def __init__(s, nc, es):
        s.nc = nc
        s.eng = {'pe': nc.tensor, 'act': nc.scalar, 'dve': nc.vector,
                 'pool': nc.gpsimd, 'sp': nc.sync}
        s.sem = {n: es.enter_context(nc.semaphore('s_' + n)) for n in s.ENG + s.GRP}
        s.reset()

    def reset(s):
        s.cnt = {n: 0 for n in s.sem}
        s.cnt['pe'] = 8
        s.cnt['act'] = 8
        s.waited = {e: {p: 0 for p in s.sem} for e in s.ENG}
        s.lastw = {}
        s.readers = {}

    def _wait(s, e, p, c):
        if c <= 0:
            return
        if p == e and e in ('pe', 'sp'):
            return
        if s.waited[e][p] < c:
            s.eng[e].wait_ge(s.sem[p], c)
            s.waited[e][p] = c

    def _deps(s, e, R, W):
        need = {}
        def add(pc):
            p, c = pc
            if c is None:
                c = s.cnt[p]
            need[p] = max(need.get(p, 0), c)
        for r in R:
            if r in s.lastw:
                add(s.lastw[r])
        for w in W:
            if w in s.lastw:
                add(s.lastw[w])
            for pc in s.readers.get(w, ()):
                add(pc)
        for p, c in need.items():
            s._wait(e, p, c)

    def op(s, e, fn, R=(), W=()):
        s._deps(e, R, W)
        ins = fn(s.eng[e])
        s.cnt[e] += 1
        ins.then_inc(s.sem[e], 1)
        for w in W:
            s.lastw[w] = (e, s.cnt[e])
            s.readers[w] = []
        for r in R:
            s.readers.setdefault(r, []).append((e, s.cnt[e]))
        return ins

    def dma(s, q, grp, out, in_, R=(), W=(), slow=False):
        s._deps(q, R, W)
        dyn = getattr(s, '_dyn', False)
        s._dyn = False
        before = s._sp_regs() if dyn else None
        ins = s.eng[q].dma_start(out=out, in_=in_, allow_slow_non_contiguous=True) if slow else s.eng[q].dma_start(out=out, in_=in_)
        if dyn:
            RH = type(s.sp_ctr)
            for n in s._sp_regs() - before:
                s.nc.sync.free_register(RH(name=n, engine=mybir.EngineType.SP))
        s.cnt[grp] += 16
        ins.then_inc(s.sem[grp], 16)
        for w in W:
            s.lastw[w] = (grp, None)
            s.readers[w] = []
        for r in R:
            s.readers.setdefault(r, []).append((grp, None))
        return ins

    def ld(s, out, in_, W, grp='l0', slow=False):
        return s.dma('sp', grp, out, in_, (), W, slow=slow)

    def st(s, out, in_, R):
        return s.dma('sp', 'st', out, in_, R, ())

    def wld(s, out, in_, W):
        return s.dma('pool', 'w', out, in_, (), W)

    def sync_all(s):
        nc = s.nc
        for g in ('l0', 'l1', 'st', 'w'):
            if s.cnt[g] > 0:
                nc.sync.wait_ge(s.sem[g], s.cnt[g])
        nc.all_engine_barrier()
        for n in s.sem:
            nc.sync.sem_clear(s.sem[n])
        nc.sync.sem_inc(s.sem['pe'], 8)
        nc.sync.sem_inc(s.sem['act'], 8)
        nc.all_engine_barrier()
        s.reset()

    def setup_regs(s):
        nc = s.nc
        s.ctr = nc.alloc_registers("k_ctr", engines=mybir.ALL_ENGINES)
        s.sp_ctr = [h for h in s.ctr if h.engine == mybir.EngineType.SP][0]
        s.sp_tmp = [nc.sync.alloc_register("k_spt%d" % j) for j in range(2)]
        s.ti = 0
        s.ctr2 = nc.alloc_registers("k_ctr2", engines=mybir.ALL_ENGINES)
        s.pe_ctr2 = [h for h in s.ctr2 if h.engine == mybir.EngineType.PE][0]
        s.act_ctr2 = [h for h in s.ctr2 if h.engine == mybir.EngineType.Activation][0]
        s.pe_t = [nc.tensor.alloc_register("k_pet%d" % j) for j in range(2)]
        s.act_t = [nc.scalar.alloc_register("k_actt%d" % j) for j in range(2)]

    def _regs_of(s, et):
        return {a.name for a in s.nc.allocations if type(a).__name__ == "Register" and a.engine == et and a.allocated}

    class _Reclaim:
        def __init__(r, k, e):
            r.k, r.e = k, e
            r.et = {'pe': mybir.EngineType.PE, 'act': mybir.EngineType.Activation, 'sp': mybir.EngineType.SP}[e]
        def __enter__(r):
            r.before = r.k._regs_of(r.et)
        def __exit__(r, *a):
            RH = type(r.k.sp_ctr)
            for n in r.k._regs_of(r.et) - r.before:
                r.k.eng[r.e].free_register(RH(name=n, engine=r.et))
            return False

    def reclaim(s, e):
        return K._Reclaim(s, e)

    def inner(s, n, body):
        nc = s.nc
        lid = nc.next_id()
        ls, le = "kin_%d_loop" % lid, "kin_%d_end" % lid
        nc.regs_mov(s.ctr2, 0)
        nc.br(ls, engines=mybir.ALL_ENGINES)
        with nc.body(ls, valid_engines=mybir.ALL_ENGINES):
            body()
            nc.regs_alu(s.ctr2, s.ctr2, 1, op=ALU.add)
            nc.br_lt(s.ctr2, n, on_true=ls, on_false=le, engines=mybir.ALL_ENGINES)
        nc.switch_bb(le)

    def _sp_regs(s):
        return {a.name for a in s.nc.allocations
                if type(a).__name__ == "Register" and a.engine == mybir.EngineType.SP and a.allocated}

    def D(s, ap0, step):
        t = s.sp_tmp[s.ti % 2]
        s.ti += 1
        s.nc.sync.reg_mul(t, s.sp_ctr, int(step))
        s.nc.sync.reg_add(t, t, int(ap0.offset))
        s._dyn = True
        return bass.AP(ap0.tensor, t, [list(p) for p in ap0.ap])

    def loop(s, n, body):
        nc = s.nc
        s.sync_all()
        lid = nc.next_id()
        ls, le = "kloop_%d_loop" % lid, "kloop_%d_end" % lid
        nc.regs_mov(s.ctr, 0)
        nc.br(ls, engines=mybir.ALL_ENGINES)
        with nc.body(ls, valid_engines=mybir.ALL_ENGINES):
            body()
            s.sync_all()
            nc.regs_alu(s.ctr, s.ctr, 1, op=ALU.add)
            nc.br_lt(s.ctr, n, on_true=ls, on_false=le, engines=mybir.ALL_ENGINES)
        nc.switch_bb(le)


def wload(k, dst, src, nk, ncols, rows=128, stage=None):
    n = 0
    for kc in range(nk):
        c0 = 0
        while c0 < ncols:
            c1 = min(ncols, c0 + 1024)
            par = n % 2
            k.dma('sp', ('l1', 'w')[par], stage[0:rows, par, 0:c1 - c0], src[kc * rows:(kc + 1) * rows, c0:c1], (), [('wstg', par)])
            eng = 'dve' if n % 2 == 0 else 'act'
            if eng == 'dve':
                k.op('dve', lambda e: e.tensor_copy(out=dst[:, kc, c0:c1], in_=stage[0:rows, par, 0:c1 - c0]), R=[('wstg', par)], W=[('wt', id(dst))])
            else:
                k.op('act', lambda e: e.activation(out=dst[:, kc, c0:c1], in_=stage[0:rows, par, 0:c1 - c0], func=AF.Copy), R=[('wstg', par)], W=[('wt', id(dst))])
            n += 1
            c0 = c1


def host_tables(SEG, joined):
    T = 2 * SEG
    NT = T // 512
    pos = np.arange(T) if joined else (np.arange(T) % SEG)
    row = (pos // 64).astype(np.float32)
    col = (pos % 64).astype(np.float32)
    inv = (10000.0 ** (-np.arange(0, 32, 2, dtype=np.float32) / 32.0)).astype(np.float32)
    ang = np.concatenate([row[:, None] * inv, col[:, None] * inv], axis=-1)
    cosf = np.repeat(np.cos(ang), 2, axis=1)
    sinf = np.repeat(np.sin(ang), 2, axis=1)
    C = np.tile(cosf.T, (2, 1)).astype(np.float32)
    S = np.tile(sinf.T, (2, 1)).astype(np.float32)
    flg = np.zeros((NT, 128, 4), np.float32)
    for t in range(NT):
        fl = 0.0 if (t == 0 or (not joined and t == NT // 2)) else 1.0
        fr = 0.0 if (t == NT - 1 or (not joined and t == NT // 2 - 1)) else 1.0
        flg[t, :, 0] = fl
        flg[t, :, 1] = fr
        flg[t, :, 2] = 0.0 if fl else NEG
        flg[t, :, 3] = 0.0 if fr else NEG
    mb = np.zeros((NT, 128, 4), np.float32)
    if not joined:
        for t in range(NT):
            sg = 0 if t < NT // 2 else 1
            mb[t, :, 1 - sg] = NEG
    return C, S, flg, mb


def const_tables():
    ident = np.eye(128, dtype=np.float32)
    R = np.zeros((128, 128), np.float32)
    for i in range(64):
        R[2 * i + 1, 2 * i] = -1.0
        R[2 * i, 2 * i + 1] = 1.0
    SH = np.zeros((128, 128), np.float32)
    for m in range(64):
        SH[m + 64, m] = 1.0
    BO = np.zeros((128, 128), np.float32)
    BO[:64, :64] = 1.0
    BO[64:, 64:] = 1.0
    k = np.arange(128)[:, None]
    q = np.arange(128)[None, :]
    M = np.zeros((128, 8, 3, 128), np.float32)
    for h in range(8):
        slope = 2.0 ** (-(h + 1))
        for d in range(3):
            dist = np.abs(128 * (d - 1) + q - k)
            M[:, h, d, :] = np.where(dist <= 128, np.exp(-slope * dist), 0.0)
    return ident, R, SH, BO, M


def w_in_perm_index():
    pair = lambda off: np.concatenate([np.concatenate([off + c * 64 + np.arange(64), off + (c + 4) * 64 + np.arange(64)]) for c in range(4)])
    idx = np.concatenate([pair(0), 512 + np.arange(128), pair(768), 1280 + np.arange(128),
                          640 + np.arange(128), 1408 + np.arange(128), np.arange(1536, NIN)])
    assert idx.shape[0] == NIN and len(set(idx.tolist())) == NIN
    return idx


def pack_small(inp, L):
    pv = np.zeros((L, 128, NPV), np.float32)
    fm = lambda a, n: a.reshape(n, 128).T
    for l in range(L):
        for kk in range(4):
            pv[l, :, kk * 8:(kk + 1) * 8] = fm(inp['conv_c_w'][l, kk], 8)
        pv[l, :, 32:40] = fm(inp['conv_c_b'][l], 8)
        for d in range(2):
            pv[l, :, 40 + d * 8:48 + d * 8] = fm(inp['b_rec_gate'][l, d], 8)
            pv[l, :, 56 + d * 8:64 + d * 8] = fm(inp['b_in_gate'][l, d], 8)
            pv[l, :, 72 + d * 8:80 + d * 8] = fm(inp['lru_lambda'][l, d], 8)
        for kk in range(3):
            pv[l, :, 88 + kk * 24:88 + (kk + 1) * 24] = fm(inp['conv_f_w'][l, kk], 24)
        pv[l, :, 160:184] = fm(inp['conv_f_b'][l], 24)
        pv[l, :, 184] = np.tile(inp['q_norm_b'][l], 2)
        pv[l, :, 185] = np.tile(inp['k_norm_b'][l], 2)
        pv[l, :, 186:194] = inp['sink_a'][l][None, :]
    bc = np.zeros((L, 6, 128, D), np.float32)
    for l in range(L):
        for j, nm in enumerate(['ln1_g', 'ln1_b', 'ln2_g', 'ln2_b', 'ln3_g', 'ln3_b']):
            bc[l, j] = inp[nm][l][None, :]
    return pv, bc


def build(SEG, L, debug=False, upto=99, p1cut=4, dense=False):
    T = 2 * SEG
    TP = T + 2 * PO
    NT = T // 512
    NT2 = T // 256
    nc = bass.Bass("TRN2", target_bir_lowering=False)
    ein = lambda n, sh, dt=F32: nc.dram_tensor(n, sh, dt, kind="ExternalInput").ap()
    scr = lambda n, sh, dt: nc.dram_tensor(n, sh, dt, kind=("ExternalOutput" if debug else "Internal")).ap()
    x_d = ein("x", [T, D]); mem_d = ein("mem", [2, MEM, D])
    w_in = ein("w_in", [L, D, NIN]); w_rg = ein("w_rg", [L, 2, 8, 128, 128]); w_ig = ein("w_ig", [L, 2, 8, 128, 128])
    w_bra = ein("w_br_a", [L, 512, D]); w_brb = ein("w_br_b", [L, 512, D]); w_brc = ein("w_br_c", [L, D, D])
    w_o = ein("w_out", [L, D, D]); w_cq = ein("w_cq", [L, D, 512]); w_ckv = ein("w_ckv", [L, D, D])
    w_co = ein("w_co", [L, 512, D]); w_up = ein("w_up", [L, D, 2 * DFF]); w_dn = ein("w_down", [L, DFF, D])
    pv_d = ein("pv", [L, 128, NPV]); bc_d = ein("bc", [L, 6, 128, D])
    C_d = ein("ropeC", [128, T]); S_d = ein("ropeS", [128, T])
    flg_d = ein("flg", [NT, 128, 4]); mb_d = ein("mb", [NT, 128, 4])
    id_d = ein("ident", [128, 128]); R_d = ein("Rm", [128, 128]); SH_d = ein("SH", [128, 128])
    BO_d = ein("BO", [128, 128]); M_d = ein("Mw", [128, 8, 3, 128])
    y_d = nc.dram_tensor("y", [T, D], F32, kind="ExternalOutput").ap()

    XT = scr("XT", [8, 128, TP], BF16); XN = scr("XN", [T, D], F32)
    X2 = scr("X2", [T, D], F32); X2T = scr("X2T", [8, 128, TP], BF16)
    KA = scr("KA", [2, 64, TP], BF16); VA = scr("VA", [TP, 2, 128], BF16)
    KB = scr("KB", [2, 64, TP], BF16); VB = scr("VB", [TP, 2, 128], BF16)
    QA = scr("QA", [8, 64, T], BF16); QB = scr("QB", [8, 64, T], BF16)
    GC = scr("GC", [8, 128, T], BF16); GT = scr("GT", [24, 128, T], BF16)
    XC = scr("XC", [8, 128, TP], F32); HF = scr("HF", [8, 128, T], F32)
    OC = scr("OC", [8, 128, T], BF16); OA = scr("OA", [8, 64, T], BF16); OB = scr("OB", [8, 64, T], BF16)
    HT = scr("HT", [24, 128, T], BF16)

    es = ExitStack()
    k = K(nc, es)
    k.setup_regs()
    Dy = k.D
    sb = lambda n, sh, dt=F32: es.enter_context(nc.sbuf_tensor(n + "_sb", sh, dt))
    ps = es.enter_context(nc.psum_tensor("ps", [128, 8, 512], F32))
    ident = sb("ident", [128, 128]); Rm = sb("Rm", [128, 128]); SH = sb("SH", [128, 128])
    BO = sb("BO", [128, 128], BF16); ones = sb("ones", [128, 128], BF16)
    memT = sb("memT", [128, 2, 8, MEM], BF16)
    kcT = sb("kcT", [128, 2, 4, MEM], BF16); vcs = sb("vcs", [128, 2, 2, 512], BF16)
    pv = sb("pv", [128, NPV]); cn = sb("cn", [128, 32]); esk = sb("esk", [128, 8])
    wstg = sb("wstg", [128, 2, 1024])
    Mt = sb("Mt", [128, 8, 3, 128]); zer = sb("zer", [128, 128], BF16); onesw = sb("onesw", [128, 512], BF16)
    k.ld(ident[:], id_d, W=['c']); k.ld(Rm[:], R_d, W=['c']); k.ld(SH[:], SH_d, W=['c'])
    k.dma('pool', 'l1', BO[:], BO_d, (), ['c'])
    k.op('pool', lambda e: e.memset(ones[:], 1.0), W=['ones'])
    k.op('pool', lambda e: e.memset(zer[:], 0.0), W=['zer'])
    k.op('pool', lambda e: e.memset(onesw[:], 1.0), W=['onesw'])
    k.ld(Mt[:], M_d, W=['c'])

    def tposes(src, nsub, dst, srck, dstk, bank0=0):
        for kc in range(8):
            b = bank0 + (kc % 4)
            for sub in range(nsub):
                k.op('pe', lambda e: e.transpose(out=ps[:, b, sub * 128:(sub + 1) * 128],
                                                 in_=src[:, sub, kc * 128:(kc + 1) * 128], identity=ident[:]),
                     R=[srck, 'c'], W=[('ps', b)])
            if kc % 2 == 0:
                k.op('act', lambda e: e.activation(out=dst[:, kc, :], in_=ps[:, b, 0:nsub * 128], func=AF.Copy),
                     R=[('ps', b)], W=[dstk])
            else:
                k.op('dve', lambda e: e.tensor_copy(out=dst[:, kc, :], in_=ps[:, b, 0:nsub * 128]),
                     R=[('ps', b)], W=[dstk])

    with ExitStack() as p0:
        psb = lambda n, sh, dt=F32: p0.enter_context(nc.sbuf_tensor(n + "_sb", sh, dt))
        zt = psb("zt", [128, 8, PO], F32); ztb = psb("ztb", [128, 8, PO], BF16)
        xs = psb("p0xs", [128, 4, D]); xTs = psb("p0xT", [128, 8, 512], BF16)
        ms = psb("p0ms", [128, 2, D])
        k.op('dve', lambda e: e.memset(zt[:], 0.0), W=['zt'])
        k.op('pool', lambda e: e.memset(ztb[:], 0.0), W=['ztb'])
        for off in (0, PO + T):
            k.st(XC[:, :, off:off + PO].rearrange("k p t -> p k t"), zt[:], R=['zt'])
            k.st(X2T[:, :, off:off + PO].rearrange("k p t -> p k t"), ztb[:], R=['ztb'])
            k.st(XT[:, :, off:off + PO].rearrange("k p t -> p k t"), ztb[:], R=['ztb'])
            k.st(KA[:, :, off:off + PO].rearrange("j p t -> p j t"), ztb[0:64, 0:2, :], R=['ztb'])
            k.st(VA[off:off + PO, :, :].rearrange("(s p) j c -> p s (j c)", p=128), ztb[:, 0:4, 0:256], R=['ztb'])
        for sg in range(2):
            k.sync_all()
            k.ld(ms[:], mem_d[sg].rearrange("(s p) f -> p s f", p=128), W=['ms'])
            tposes(ms, 2, memT[:, sg], 'ms', 'memT')

        def body0():
            k.ld(xs[:], Dy(x_d[0:512, :].rearrange("(s p) f -> p s f", p=128), 512 * D), W=['xs'])
            tposes(xs, 4, xTs, 'xs', 'xTs')
            k.st(Dy(XT[:, :, PO:PO + 512].rearrange("k p t -> p k t"), 512), xTs[:], R=['xTs'])
        k.loop(NT, body0)

    for l in range(L):
        xres = x_d if l == 0 else XN
        last = (l == L - 1)
        with ExitStack() as pc:
            psb = lambda n, sh, dt=F32: pc.enter_context(nc.sbuf_tensor(f"{n}_sb{l}", sh, dt))
            wkv = psb("wkv", [128, 8, D], BF16)
            k.ld(pv[:], pv_d[l], W=['pv'])
            wload(k, wkv, w_ckv[l], 8, D, stage=wstg)
            k.op('act', lambda e: e.activation(out=cn[:, 0:16], in_=pv[:, 72:88], func=AF.Exp, scale=-1.0), R=['pv'], W=['cn'])
            k.op('act', lambda e: e.activation(out=cn[:, 0:16], in_=cn[:, 0:16], func=AF.Ln, bias=1.0), R=['cn'], W=['cn'])
            k.op('dve', lambda e: e.tensor_scalar(out=cn[:, 16:32], in0=cn[:, 0:16], scalar1=-16.0, scalar2=None, op0=ALU.mult), R=['cn'], W=['cn2'])
            k.op('dve', lambda e: e.tensor_scalar(out=cn[:, 0:16], in0=cn[:, 0:16], scalar1=-8.0, scalar2=None, op0=ALU.mult), R=['cn', 'cn2'], W=['cn'])
            k.op('act', lambda e: e.activation(out=esk[:], in_=pv[:, 186:194], func=AF.Exp), R=['pv'], W=['esk'])
            for sg in range(2):
                for h in range(4):
                    b = h
                    for kc in range(8):
                        k.op('pe', lambda e: e.matmul(ps[:, b, 0:MEM], lhsT=wkv[:, kc, h * 128:(h + 1) * 128], rhs=memT[:, sg, kc, :],
                                                      start=(kc == 0), stop=(kc == 7)), R=[('wt', id(wkv)), 'memT'], W=[('ps', b)])
                    k.op('act', lambda e: e.activation(out=kcT[:, sg, h, :], in_=ps[:, b, 0:MEM], func=AF.Copy), R=[('ps', b)], W=['kcT'])
                for mc in range(2):
                    b = 4 + mc
                    for kc in range(8):
                        k.op('pe', lambda e: e.matmul(ps[:, b, :], lhsT=memT[:, sg, kc, mc * 128:(mc + 1) * 128], rhs=wkv[:, kc, 512:1024],
                                                      start=(kc == 0), stop=(kc == 7)), R=[('wt', id(wkv)), 'memT'], W=[('ps', b)])
                    k.op('dve', lambda e: e.tensor_copy(out=vcs[:, sg, mc, :], in_=ps[:, b, :]), R=[('ps', b)], W=['vcs'])
                k.sync_all()

        if upto >= 1:
          with ExitStack() as p1:
            psb = lambda n, sh, dt=F32: p1.enter_context(nc.sbuf_tensor(f"{n}_sb{l}", sh, dt))
            win = psb("win", [128, 8, NIN], BF16)
            xT = psb("p1xT", [128, 8, 256], BF16); Ct = psb("p1C", [128, 256]); St = psb("p1S", [128, 256])
            qa_st = psb("qa_st", [128, 4, 256], BF16); qb_st = psb("qb_st", [128, 4, 256], BF16)
            ka_st = psb("ka_st", [128, 256], BF16); kb_st = psb("kb_st", [128, 256], BF16)
            xc_st = psb("xc_st", [128, 8, 256]); gc_st = psb("gc_st", [128, 8, 256], BF16)
            g_st = psb("g_st", [128, 24, 256], BF16)
            v_st = psb("v_st", [128, 2, 2, 2, 128], BF16)
            sq = psb("p1sq", [128, 2, 256], BF16); rs = psb("p1rs", [128, 2, 256]); qn = psb("p1qn", [128, 2, 256])
            t1 = psb("p1t1", [128, 2, 256]); t2 = psb("p1t2", [128, 2, 256])
            wload(k, win, w_in[l], 8, NIN, stage=wstg)
            k.op('pool', lambda e: e.memset(v_st[:], 1.0), W=['v_st'])
            WK = ('wt', id(win))
            slot = [0]

            def nslot():
                s_ = slot[0] % 8
                slot[0] += 1
                return s_, ps[:, s_, 0:256]

            def proj(lhs_fn):
                s_, pt = nslot()
                for kc in range(8):
                    k.op('pe', lambda e: e.matmul(pt, lhsT=lhs_fn(kc), rhs=xT[:, kc, :], start=(kc == 0), stop=(kc == 7)),
                         R=[WK, 'xT'], W=[('ps', s_)])
                return s_, pt

            def pair_lhs(off):
                return lambda c: (lambda kc: win[:, kc, off + c * 128:off + (c + 1) * 128])

            def normrope(s_, pt, gcol, dst, dstk, par):
                k.op('act', lambda e: e.activation(out=sq[:, par, :], in_=pt, func=AF.Square), R=[('ps', s_)], W=[('sq', par)])
                s2, p2 = nslot()
                k.op('pe', lambda e: e.matmul(p2, lhsT=BO[:], rhs=sq[:, par, :], start=True, stop=True), R=[('sq', par), 'c'], W=[('ps', s2)])
                k.op('act', lambda e: e.activation(out=rs[:, par, :], in_=p2, func=AF.Sqrt, scale=1.0 / 64.0, bias=RMS_EPS), R=[('ps', s2)], W=[('rs', par)])
                k.op('dve', lambda e: e.reciprocal(out=rs[:, par, :], in_=rs[:, par, :]), R=[('rs', par)], W=[('rs', par)])
                k.op('dve', lambda e: e.scalar_tensor_tensor(out=qn[:, par, :], in0=pt, scalar=pv[:, gcol:gcol + 1], in1=rs[:, par, :],
                                                             op0=ALU.mult, op1=ALU.mult), R=[('ps', s_), ('rs', par), 'pv'], W=[('qn', par)])
                s3, p3 = nslot()
                k.op('pe', lambda e: e.matmul(p3, lhsT=Rm[:], rhs=qn[:, par, :], start=True, stop=True), R=[('qn', par), 'c'], W=[('ps', s3)])
                k.op('dve', lambda e: e.tensor_tensor(out=t1[:, par, :], in0=qn[:, par, :], in1=Ct[:], op=ALU.mult), R=[('qn', par), 'CS'], W=[('t1', par)])
                k.op('dve', lambda e: e.tensor_tensor(out=t2[:, par, :], in0=p3, in1=St[:], op=ALU.mult), R=[('ps', s3), 'CS'], W=[('t2', par)])
                k.op('pool', lambda e: e.tensor_tensor(out=dst, in0=t1[:, par, :], in1=t2[:, par, :], op=ALU.add), R=[('t1', par), ('t2', par)], W=[dstk])

            def body1():
                slot[0] = 0
                k.ld(xT[:], Dy(XT[:, :, PO:PO + 256].rearrange("k p t -> p k t"), 256), W=['xT'])
                k.ld(Ct[:], Dy(C_d[:, 0:256], 256), W=['CS'])
                k.ld(St[:], Dy(S_d[:, 0:256], 256), W=['CS'])
                if p1cut < 0.15:
                    return
                for c in (range(4) if p1cut >= 0.3 else range(1)):
                    s_, pt = proj(pair_lhs(0)(c))
                    k.op('dve', lambda e: e.tensor_copy(out=qa_st[:, c, :], in_=pt), R=[('ps', s_)], W=['qa_st'])
                if p1cut < 0.3:
                    return
                s_, pt = proj(lambda kc: win[:, kc, 512:640])
                k.op('dve', lambda e: e.tensor_copy(out=ka_st[:], in_=pt), R=[('ps', s_)], W=['ka_st'])
                if p1cut < 1:
                    return
                for c in (range(4) if p1cut >= 2 else ()):
                    s_, pt = proj(pair_lhs(640)(c))
                    normrope(s_, pt, 184, qb_st[:, c, :], 'qb_st', c % 2)
                if p1cut >= 2:
                    s_, pt = proj(lambda kc: win[:, kc, 1152:1280])
                    normrope(s_, pt, 185, kb_st[:], 'kb_st', 0)
                for sub in (range(2) if p1cut >= 3 else ()):
                    s_, pt = nslot()
                    for kc in range(8):
                        k.op('pe', lambda e: e.matmul(pt, lhsT=xT[:, kc, sub * 128:(sub + 1) * 128],
                                                      rhs=win[:, kc, 1280:1536],
                                                      start=(kc == 0), stop=(kc == 7)), R=[WK, 'xT'], W=[('ps', s_)])
                    k.op('dve', lambda e: e.tensor_copy(out=v_st[:, sub, :, :, 0:64], in_=pt.rearrange("p (a j d) -> p a j d", a=2, j=2, d=64)),
                         R=[('ps', s_)], W=['v_st'])
                for n in (range(8) if p1cut >= 4 else ()):
                    s_, pt = proj(lambda kc: win[:, kc, 1536 + n * 128:1536 + (n + 1) * 128])
                    k.op('dve', lambda e: e.tensor_copy(out=xc_st[:, n, :], in_=pt), R=[('ps', s_)], W=['xc_st'])
                for n in (range(8) if p1cut >= 4 else ()):
                    s_, pt = proj(lambda kc: win[:, kc, 2560 + n * 128:2560 + (n + 1) * 128])
                    k.op('act', lambda e: e.activation(out=gc_st[:, n, :], in_=pt, func=AF.Gelu_apprx_tanh), R=[('ps', s_)], W=['gc_st'])
                for n in (range(24) if p1cut >= 4 else ()):
                    s_, pt = proj(lambda kc: win[:, kc, 3584 + n * 128:3584 + (n + 1) * 128])
                    k.op('act', lambda e: e.activation(out=g_st[:, n, :], in_=pt, func=AF.Sigmoid), R=[('ps', s_)], W=['g_st'])
                k.st(Dy(QA[0:4, :, 0:256].rearrange("c p t -> p c t"), 256), qa_st[0:64], R=['qa_st'])
                k.st(Dy(QA[4:8, :, 0:256].rearrange("c p t -> p c t"), 256), qa_st[64:128], R=['qa_st'])
                k.st(Dy(KA[:, :, PO:PO + 256].rearrange("j p t -> (j p) t"), 256), ka_st[:], R=['ka_st'])
                if p1cut >= 2:
                    k.st(Dy(QB[0:4, :, 0:256].rearrange("c p t -> p c t"), 256), qb_st[0:64], R=['qb_st'])
                    k.st(Dy(QB[4:8, :, 0:256].rearrange("c p t -> p c t"), 256), qb_st[64:128], R=['qb_st'])
                    k.st(Dy(KB[:, :, PO:PO + 256].rearrange("j p t -> (j p) t"), 256), kb_st[:], R=['kb_st'])
                if p1cut >= 3:
                    k.st(Dy(VA[PO:PO + 256, :, :].rearrange("(s p) j c -> p s j c", p=128), 256 * 256), v_st[:, :, 0, :, :], R=['v_st'])
                    k.st(Dy(VB[PO:PO + 256, :, :].rearrange("(s p) j c -> p s j c", p=128), 256 * 256), v_st[:, :, 1, :, :], R=['v_st'])
                if p1cut >= 4:
                    k.st(Dy(XC[:, :, PO:PO + 256].rearrange("k p t -> p k t"), 256), xc_st[:], R=['xc_st'])
                    k.st(Dy(GC[:, :, 0:256].rearrange("k p t -> p k t"), 256), gc_st[:], R=['gc_st'])
                    for g3 in range(3):
                        k.st(Dy(GT[g3 * 8:(g3 + 1) * 8, :, 0:256].rearrange("k p t -> p k t"), 256), g_st[:, g3 * 8:(g3 + 1) * 8, :], R=['g_st'])
            if p1cut > 0:
                k.loop(NT2, body1)

        if upto >= 2:
          for dr in range(2):
            with ExitStack() as p2:
                psb = lambda n, sh, dt=F32: p2.enter_context(nc.sbuf_tensor(f"{n}_sb{l}_{dr}", sh, dt))
                wr = psb("wr", [128, 8, 128], BF16); wi = psb("wi", [128, 8, 128], BF16)
                xcl = psb("xcl", [128, 8, 515]); fl = psb("p2fl", [128, 4])
                u = psb("p2u", [128, 8, 512]); ub = psb("p2ub", [128, 8, 512], BF16)
                r = psb("p2r", [128, 8, 512]); ii = psb("p2i", [128, 8, 512])
                a = psb("p2a", [128, 8, 512]); m = psb("p2m", [128, 8, 512])
                hst = psb("p2h", [128, 8, 512]); carry = psb("p2c", [128, 8, 1])
                if dr == 1:
                    hfl = psb("p2hf", [128, 8, 512]); gcl = psb("p2gc", [128, 8, 512], BF16)
                    ocs = psb("p2oc", [128, 8, 512], BF16)
                wload(k, wr, w_rg[l, dr].rearrange("n c d -> (n c) d"), 8, 128, stage=wstg)
                wload(k, wi, w_ig[l, dr].rearrange("n c d -> (n c) d"), 8, 128, stage=wstg)
                k.op('dve', lambda e: e.memset(carry[:], 0.0), W=['carry'])
                WR, WI = ('wt', id(wr)), ('wt', id(wi))
                rev = (lambda ap: ap[:, ::-1]) if dr == 1 else (lambda ap: ap)

                tb = 0 if dr == 0 else (NT - 1) * 512
                sg_ = 1 if dr == 0 else -1

                def body2():
                    k.ld(xcl[:], Dy(XC[:, :, PO + tb - 1:PO + tb + 514].rearrange("k p t -> p k t"), sg_ * 512), W=['xcl'])
                    k.ld(fl[:], Dy(flg_d[tb // 512:tb // 512 + 1, :, :].rearrange("o p f -> p (o f)"), sg_ * 512), W=['fl'])
                    if dr == 1:
                        k.ld(hfl[:], Dy(HF[:, :, tb:tb + 512].rearrange("k p t -> p k t"), sg_ * 512), W=['hfl'])
                        k.ld(gcl[:], Dy(GC[:, :, tb:tb + 512].rearrange("k p t -> p k t"), sg_ * 512), W=['gcl'])
                    k.op('dve', lambda e: e.tensor_scalar(out=xcl[:, :, 0:1], in0=xcl[:, :, 0:1], scalar1=fl[:, 0:1], scalar2=None, op0=ALU.mult),
                         R=['xcl', 'fl'], W=['xcl'])
                    k.op('dve', lambda e: e.tensor_scalar(out=xcl[:, :, 513:515], in0=xcl[:, :, 513:515], scalar1=fl[:, 1:2], scalar2=None, op0=ALU.mult),
                         R=['xcl', 'fl'], W=['xcl'])
                    for n in range(8):
                        k.op('dve', lambda e: e.tensor_scalar(out=u[:, n, :], in0=xcl[:, n, 0:512], scalar1=pv[:, n:n + 1], scalar2=pv[:, 32 + n:33 + n],
                                                              op0=ALU.mult, op1=ALU.add), R=['xcl', 'pv'], W=[('u', n)])
                        for tap in (1, 2, 3):
                            k.op('dve', lambda e: e.scalar_tensor_tensor(out=u[:, n, :], in0=xcl[:, n, tap:tap + 512], scalar=pv[:, tap * 8 + n:tap * 8 + n + 1],
                                                                         in1=u[:, n, :], op0=ALU.mult, op1=ALU.add), R=['xcl', 'pv', ('u', n)], W=[('u', n)])
                        k.op('pool', lambda e: e.tensor_copy(out=ub[:, n, :], in_=u[:, n, :]), R=[('u', n)], W=[('ub', n)])
                        b0, b1 = (2 * n) % 8, (2 * n + 1) % 8
                        k.op('pe', lambda e: e.matmul(ps[:, b0, :], lhsT=wr[:, n, :], rhs=ub[:, n, :], start=True, stop=True), R=[WR, ('ub', n)], W=[('ps', b0)])
                        k.op('pe', lambda e: e.matmul(ps[:, b1, :], lhsT=wi[:, n, :], rhs=ub[:, n, :], start=True, stop=True), R=[WI, ('ub', n)], W=[('ps', b1)])
                        c_r, c_i = 40 + dr * 8 + n, 56 + dr * 8 + n
                        k.op('act', lambda e: e.activation(out=r[:, n, :], in_=ps[:, b0, :], func=AF.Sigmoid, bias=pv[:, c_r:c_r + 1]), R=[('ps', b0), 'pv'], W=[('r', n)])
                        k.op('act', lambda e: e.activation(out=ii[:, n, :], in_=ps[:, b1, :], func=AF.Sigmoid, bias=pv[:, c_i:c_i + 1]), R=[('ps', b1), 'pv'], W=[('i', n)])
                    for n in range(8):
                        c1, c2 = dr * 8 + n, 16 + dr * 8 + n
                        k.op('act', lambda e: e.activation(out=a[:, n, :], in_=r[:, n, :], func=AF.Exp, scale=cn[:, c1:c1 + 1]), R=[('r', n)], W=[('a', n)])
                        k.op('act', lambda e: e.activation(out=m[:, n, :], in_=r[:, n, :], func=AF.Exp, scale=cn[:, c2:c2 + 1]), R=[('r', n)], W=[('m', n)])
                    for n in range(8):
                        k.op('dve', lambda e: e.tensor_scalar(out=m[:, n, :], in0=m[:, n, :], scalar1=1.0, scalar2=-1.0, op0=ALU.min, op1=ALU.mult), R=[('m', n)], W=[('m', n)])
                    for n in range(8):
                        k.op('act', lambda e: e.activation(out=m[:, n, :], in_=m[:, n, :], func=AF.Sqrt, bias=1.0), R=[('m', n)], W=[('m', n)])
                    col = 0 if dr == 0 else 511
                    k.op('dve', lambda e: e.tensor_scalar(out=a[:, :, col:col + 1], in0=a[:, :, col:col + 1], scalar1=fl[:, dr:dr + 1], scalar2=None, op0=ALU.mult),
                         R=[('a', n_) for n_ in range(8)] + ['fl'], W=[('a', n_) for n_ in range(8)])
                    for n in range(8):
                        k.op('pool', lambda e: e.tensor_tensor(out=ii[:, n, :], in0=ii[:, n, :], in1=u[:, n, :], op=ALU.mult), R=[('i', n), ('u', n)], W=[('i', n)])
                        k.op('dve', lambda e: e.tensor_tensor(out=ii[:, n, :], in0=ii[:, n, :], in1=m[:, n, :], op=ALU.mult), R=[('i', n), ('m', n)], W=[('i', n)])
                        k.op('dve', lambda e: e.tensor_tensor_scan(out=rev(hst[:, n, :]), data0=rev(a[:, n, :]), data1=rev(ii[:, n, :]),
                                                                   initial=carry[:, n, :], op0=ALU.mult, op1=ALU.add),
                             R=[('a', n), ('i', n), 'carry'], W=[('h', n)])
                    ec = 511 if dr == 0 else 0
                    k.op('dve', lambda e: e.tensor_copy(out=carry[:], in_=hst[:, :, ec:ec + 1]), R=[('h', n_) for n_ in range(8)], W=['carry'])
                    if dr == 0:
                        k.st(Dy(HF[:, :, tb:tb + 512].rearrange("k p t -> p k t"), sg_ * 512), hst[:], R=[('h', n_) for n_ in range(8)])
                    else:
                        for n in range(8):
                            k.op('pool', lambda e: e.tensor_tensor(out=hst[:, n, :], in0=hst[:, n, :], in1=hfl[:, n, :], op=ALU.add), R=[('h', n), 'hfl'], W=[('h', n)])
                            k.op('pool', lambda e: e.tensor_tensor(out=ocs[:, n, :], in0=hst[:, n, :], in1=gcl[:, n, :], op=ALU.mult), R=[('h', n), 'gcl'], W=[('oc', n)])
                        k.st(Dy(OC[:, :, tb:tb + 512].rearrange("k p t -> p k t"), sg_ * 512), ocs[:], R=[('oc', n_) for n_ in range(8)])
                k.loop(NT, body2)


        if upto >= 3:
          with ExitStack() as p3:
            psb = lambda n, sh, dt=F32: p3.enter_context(nc.sbuf_tensor(f"{n}_sb{l}", sh, dt))
            q = psb("wq", [64, 8, 512], BF16); kT = psb("wkT", [64, 2, 768], BF16); v = psb("wv", [128, 6, 2, 128], BF16)
            fl = psb("wfl", [128, 4]); e_sb = psb("we", [128, 2, 384]); p_sb = psb("wp", [128, 2, 384], BF16)
            acc_sb = psb("wacc", [128, 2, 512]); rden = psb("wrd", [64, 2, 512]); oa_st = psb("woa", [64, 8, 512], BF16)

            def body3w():
                k.ld(q[:], Dy(QA[:, :, 0:512].rearrange("h p t -> p h t"), 512), W=['q'])
                k.ld(kT[:], Dy(KA[:, :, PO - 128:PO + 640].rearrange("j p t -> p j t"), 512), W=['kT'])
                k.ld(v[:], Dy(VA[PO - 128:PO + 640, :, :].rearrange("(b p) j c -> p b j c", p=128), 512 * 256), W=['v'])
                k.ld(fl[:], Dy(flg_d[0:1, :, :].rearrange("o p f -> p (o f)"), 512), W=['fl'])
                for h in range(8):
                    j = h // 4
                    accb = 4 + (h % 2)
                    hp = h % 2
                    for r in range(6):
                        lo, hi = max(r - 2, 0), min(r, 3)
                        nb = hi - lo + 1
                        nq = nb * 128
                        sbk = (h * 6 + r) % 4
                        par = (h * 6 + r) % 2
                        k.op('pe', lambda e: e.matmul(ps[:, sbk, 0:nq], lhsT=kT[:, j, r * 128:(r + 1) * 128], rhs=q[:, h, lo * 128:(hi + 1) * 128],
                                                      start=True, stop=True), R=['kT', 'q'], W=[('ps', sbk)])
                        bias = fl[:, 2:3] if r == 0 else (fl[:, 3:4] if r == 5 else 0.0)
                        k.op('act', lambda e: e.activation(out=e_sb[:, par, 0:nq], in_=ps[:, sbk, 0:nq], func=AF.Exp, scale=0.125, bias=bias),
                             R=[('ps', sbk), 'fl'], W=[('e', par)])
                        k.op('pool', lambda e: e.tensor_tensor(out=p_sb[:, par, 0:nq].rearrange("p (b q) -> p b q", b=nb),
                                                               in0=e_sb[:, par, 0:nq].rearrange("p (b q) -> p b q", b=nb),
                                                               in1=Mt[:, h, lo - r + 2:hi - r + 3, :], op=ALU.mult), R=[('e', par), 'c'], W=[('p', par)])
                        k.op('pe', lambda e: e.matmul(ps[:, accb, lo * 128:(hi + 1) * 128], lhsT=v[:, r, j, :], rhs=p_sb[:, par, 0:nq],
                                                      start=(r == 0), stop=(r == 5)), R=['v', ('p', par)], W=[('ps', accb)])
                    k.op('act', lambda e: e.activation(out=acc_sb[:, hp, :], in_=ps[:, accb, :], func=AF.Copy), R=[('ps', accb)], W=[('acc', hp)])
                    k.op('pe', lambda e: e.matmul(ps[:, 6 + hp, :], lhsT=SH[:], rhs=acc_sb[:, hp, :], start=True, stop=True), R=[('acc', hp), 'c'], W=[('ps', 6 + hp)])
                    k.op('dve', lambda e: e.tensor_scalar(out=rden[:, hp, :], in0=ps[0:64, 6 + hp, :], scalar1=esk[0:64, h:h + 1], scalar2=None, op0=ALU.add),
                         R=[('ps', 6 + hp), 'esk'], W=[('rd', hp)])
                    k.op('dve', lambda e: e.reciprocal(out=rden[:, hp, :], in_=rden[:, hp, :]), R=[('rd', hp)], W=[('rd', hp)])
                    k.op('dve', lambda e: e.tensor_tensor(out=oa_st[:, h, :], in0=acc_sb[0:64, hp, :], in1=rden[:, hp, :], op=ALU.mult),
                         R=[('acc', hp), ('rd', hp)], W=[('oa', h)])
                k.st(Dy(OA[:, :, 0:512].rearrange("h p t -> p h t"), 512), oa_st[:], R=[('oa', h_) for h_ in range(8)])
            k.loop(NT, body3w)

        if upto >= 3 and dense:
          with ExitStack() as p3:
            psb = lambda n, sh, dt=F32: p3.enter_context(nc.sbuf_tensor(f"{n}_sb{l}", sh, dt))
            NKT = T // 128
            KS = SEG // 128
            NJ = KS // 4
            kTs = psb("dkT", [64, 2, T], BF16); vs = psb("dv", [128, NKT * 256], BF16)
            q = psb("dq", [64, 8, 512], BF16); mbt = psb("dmb", [128, 4])
            p_sb = psb("dp", [128, 4, 512], BF16); acc_sb = psb("dacc", [128, 512]); rden = psb("drd", [64, 512])
            ob_st = psb("dob", [64, 8, 512], BF16)
            k.ld(kTs[:], KB[:, :, PO:PO + T].rearrange("j p t -> p j t"), W=['kTs'])
            for b0 in range(0, NKT, 8):
                k.ld(vs[:, b0 * 256:(b0 + 8) * 256].rearrange("p (b c) -> p b c", b=8),
                     VB[PO + b0 * 128:PO + (b0 + 8) * 128, :, :].rearrange("(b p) j c -> p b (j c)", p=128), W=['vs'])
            pe, act = nc.tensor, nc.scalar
            sem_pe, sem_act = k.sem['pe'], k.sem['act']

            def dynap(eng, treg, ctr, a0, step):
                eng.reg_mul(treg, ctr, int(step))
                eng.reg_add(treg, treg, int(a0.offset))
                return bass.AP(a0.tensor, treg, [list(p_) for p_ in a0.ap])

            def dynwait(eng, treg, ctr, sem_, mul, base):
                eng.reg_mul(treg, ctr, int(mul))
                eng.reg_add(treg, treg, int(base))
                eng.wait_ge(sem_, treg)

            def body3a():
                k.ld(q[:], Dy(QB[:, :, 0:512].rearrange("h p t -> p h t"), 512), W=['q'])
                k.ld(mbt[:], Dy(mb_d[0:1, :, :].rearrange("o p f -> p (o f)"), 512), W=['mbt'])
                for h in range(8):
                    j = h // 4
                    k.op('pe', lambda e: e.matmul(ps[:, 4, :], lhsT=zer[:], rhs=onesw[:], start=True, stop=False), R=['q'], W=[('ps', 4)])
                    k._deps('act', ['mbt'], [])
                    A, B = k.cnt['act'], k.cnt['pe']
                    pe.wait_ge(sem_act, A)
                    act.wait_ge(sem_pe, B)
                    for sg in range(2):
                        As, Bs = A + sg * NJ * 4, B + sg * NJ * 8

                        def inner_body():
                            for u_ in range(4):
                                with k.reclaim('pe'):
                                    dynwait(pe, k.pe_t[0], k.pe_ctr2, sem_act, 4, As + u_ - 3)
                                    a0 = kTs[:, j, (sg * KS + u_) * 128:(sg * KS + u_ + 1) * 128]
                                    pe.matmul(ps[:, u_, :], lhsT=dynap(pe, k.pe_t[1], k.pe_ctr2, a0, 4 * 128), rhs=q[:, h, :],
                                              start=True, stop=True).then_inc(sem_pe, 1)
                            for u_ in range(4):
                                with k.reclaim('pe'):
                                    dynwait(pe, k.pe_t[0], k.pe_ctr2, sem_act, 4, As + u_ + 1)
                                    c0 = (sg * KS + u_) * 256 + j * 128
                                    a0 = vs[:, c0:c0 + 128]
                                    pe.matmul(ps[:, 4, :], lhsT=dynap(pe, k.pe_t[1], k.pe_ctr2, a0, 4 * 256), rhs=p_sb[:, u_, :],
                                              start=False, stop=False).then_inc(sem_pe, 1)
                            for u_ in range(4):
                                with k.reclaim('act'):
                                    dynwait(act, k.act_t[0], k.act_ctr2, sem_pe, 8, Bs + u_ + 1)
                                act.activation(out=p_sb[:, u_, :], in_=ps[:, u_, :], func=AF.Exp, scale=0.125,
                                               bias=mbt[:, sg:sg + 1]).then_inc(sem_act, 1)
                        k.inner(NJ, inner_body)
                    k.cnt['act'] = A + 2 * NJ * 4
                    k.cnt['pe'] = B + 2 * NJ * 8
                    k.lastw[('ps', 4)] = ('pe', k.cnt['pe'])
                    k.readers[('ps', 4)] = []
                    k.op('act', lambda e: e.activation(out=acc_sb[:], in_=ps[:, 4, :], func=AF.Copy), R=[('ps', 4)], W=['dacc'])
                    k.op('pe', lambda e: e.matmul(ps[:, 5, :], lhsT=SH[:], rhs=acc_sb[:], start=True, stop=True), R=['dacc', 'c'], W=[('ps', 5)])
                    k.op('dve', lambda e: e.reciprocal(out=rden[:], in_=ps[0:64, 5, :]), R=[('ps', 5)], W=['drd'])
                    k.op('dve', lambda e: e.tensor_tensor(out=ob_st[:, h, :], in0=acc_sb[0:64, :], in1=rden[:], op=ALU.mult), R=['dacc', 'drd'], W=[('ob', h)])
                k.st(Dy(OB[:, :, 0:512].rearrange("h p t -> p h t"), 512), ob_st[:], R=[('ob', h_) for h_ in range(8)])
            k.loop(NT, body3a)

        if upto >= 4:
          with ExitStack() as p3:
            psb = lambda n, sh, dt=F32: p3.enter_context(nc.sbuf_tensor(f"{n}_sb{l}", sh, dt))
            wa = psb("mwa", [64, 8, D], BF16); wb = psb("mwb", [64, 8, D], BF16); wc = psb("mwc", [128, 8, D], BF16)
            wo = psb("mwo", [128, 8, D], BF16); wq = psb("mwq", [128, 8, 512], BF16); wco = psb("mwco", [128, 4, D], BF16)
            lng = psb("mln", [128, 4, D])
            oa = psb("moa", [64, 8, 256], BF16); ob = psb("mob", [64, 8, 256], BF16); oc = psb("moc", [128, 8, 256], BF16)
            g = psb("mg", [128, 24, 256], BF16); xr = psb("mxr", [128, 2, D])
            m0 = psb("mm0", [128, 2, 256]); m1 = psb("mm1", [128, 2, 256]); m2 = psb("mm2", [128, 2, 256])
            mrg = psb("mmrg", [128, 8, 256], BF16); x1T = psb("mx1T", [128, 8, 256], BF16); x2T = psb("mx2T", [128, 8, 256], BF16)
            qcT = psb("mqc", [128, 4, 256], BF16); pT = psb("mpT", [128, 4, 2, 256], BF16); ocT = psb("mocT", [128, 4, 256], BF16)
            rdn = psb("mrdn", [128, 4, 256]); stt = psb("mst", [128, 2, 12]); mv = psb("mmv", [128, 2, 2])
            sd = psb("msd", [128, 2, 1]); nmr = psb("mnmr", [128, 2, 1])
            wload(k, wa, w_bra[l], 8, D, rows=64, stage=wstg); wload(k, wb, w_brb[l], 8, D, rows=64, stage=wstg)
            wload(k, wc, w_brc[l], 8, D, stage=wstg); wload(k, wo, w_o[l], 8, D, stage=wstg)
            wload(k, wq, w_cq[l], 8, 512, stage=wstg); wload(k, wco, w_co[l], 4, D, stage=wstg)
            k.ld(lng[:], bc_d[l, 0:4].rearrange("j p f -> p j f"), W=['lng'])
            WKS = [('wt', id(w_)) for w_ in (wa, wb, wc, wo, wq, wco)]

            def lnorm(buf, bufk, sub, gi, lnt, lnk):
                k.op('dve', lambda e: e.bn_stats(out=stt[:, sub, 0:6], in_=buf[:, sub, 0:512]), R=[bufk], W=[('stt', sub)])
                k.op('dve', lambda e: e.bn_stats(out=stt[:, sub, 6:12], in_=buf[:, sub, 512:1024]), R=[bufk], W=[('stt', sub)])
                k.op('dve', lambda e: e.bn_aggr(out=mv[:, sub, :], in_=stt[:, sub, :]), R=[('stt', sub)], W=[('mv', sub)])
                k.op('act', lambda e: e.activation(out=sd[:, sub, :], in_=mv[:, sub, 1:2], func=AF.Sqrt, bias=LN_EPS), R=[('mv', sub)], W=[('sd', sub)])
                k.op('dve', lambda e: e.reciprocal(out=sd[:, sub, :], in_=sd[:, sub, :]), R=[('sd', sub)], W=[('sd', sub)])
                k.op('dve', lambda e: e.tensor_scalar(out=nmr[:, sub, :], in0=mv[:, sub, 0:1], scalar1=sd[:, sub, :], scalar2=-1.0, op0=ALU.mult, op1=ALU.mult),
                     R=[('mv', sub), ('sd', sub)], W=[('nmr', sub)])
                k.op('act', lambda e: e.activation(out=buf[:, sub, :], in_=buf[:, sub, :], func=AF.Identity, scale=sd[:, sub, :], bias=nmr[:, sub, :]),
                     R=[bufk, ('sd', sub), ('nmr', sub)], W=[bufk])
                k.op('pool', lambda e: e.tensor_tensor(out=buf[:, sub, :], in0=buf[:, sub, :], in1=lnt[:, gi, :], op=ALU.mult), R=[bufk, lnk], W=[bufk])
                k.op('pool', lambda e: e.tensor_tensor(out=buf[:, sub, :], in0=buf[:, sub, :], in1=lnt[:, gi + 1, :], op=ALU.add), R=[bufk, lnk], W=[bufk])

            for seg in range(2):
                tb = seg * SEG

                def body3b():
                    k.ld(oa[:], Dy(OA[:, :, tb:tb + 256].rearrange("h p t -> p h t"), 256), W=['oa'])
                    k.ld(ob[:], Dy(OB[:, :, tb:tb + 256].rearrange("h p t -> p h t"), 256), W=['ob'])
                    k.ld(oc[:], Dy(OC[:, :, tb:tb + 256].rearrange("k p t -> p k t"), 256), W=['oc'])
                    for g3 in range(3):
                        k.ld(g[:, g3 * 8:(g3 + 1) * 8, :], Dy(GT[g3 * 8:(g3 + 1) * 8, :, tb:tb + 256].rearrange("k p t -> p k t"), 256), W=['g'])
                    k.ld(xr[:], Dy(xres[tb:tb + 256, :].rearrange("(s p) f -> p s f", p=128), 256 * D), W=['xr'])
                    for mch in range(8):
                        par = mch % 2
                        bA, bB, bC = 3 * par, 3 * par + 1, 3 * par + 2
                        cs = slice(mch * 128, (mch + 1) * 128)
                        for h in range(8):
                            k.op('pe', lambda e: e.matmul(ps[:, bA, 0:256], lhsT=wa[:, h, cs], rhs=oa[:, h, :], start=(h == 0), stop=(h == 7)), R=['oa'], W=[('ps', bA)])
                        for h in range(8):
                            k.op('pe', lambda e: e.matmul(ps[:, bB, 0:256], lhsT=wb[:, h, cs], rhs=ob[:, h, :], start=(h == 0), stop=(h == 7)), R=['ob'], W=[('ps', bB)])
                        for kc in range(8):
                            k.op('pe', lambda e: e.matmul(ps[:, bC, 0:256], lhsT=wc[:, kc, cs], rhs=oc[:, kc, :], start=(kc == 0), stop=(kc == 7)), R=['oc'], W=[('ps', bC)])
                        k.op('dve', lambda e: e.tensor_tensor(out=m0[:, par, :], in0=ps[:, bA, 0:256], in1=g[:, mch, :], op=ALU.mult), R=[('ps', bA), 'g'], W=[('m0', par)])
                        k.op('dve', lambda e: e.tensor_tensor(out=m1[:, par, :], in0=ps[:, bB, 0:256], in1=g[:, 8 + mch, :], op=ALU.mult), R=[('ps', bB), 'g'], W=[('m1', par)])
                        k.op('dve', lambda e: e.tensor_tensor(out=m2[:, par, :], in0=ps[:, bC, 0:256], in1=g[:, 16 + mch, :], op=ALU.mult), R=[('ps', bC), 'g'], W=[('m2', par)])
                        k.op('pool', lambda e: e.tensor_tensor(out=m0[:, par, :], in0=m0[:, par, :], in1=m1[:, par, :], op=ALU.add), R=[('m0', par), ('m1', par)], W=[('m0', par)])
                        k.op('pool', lambda e: e.tensor_tensor(out=mrg[:, mch, :], in0=m0[:, par, :], in1=m2[:, par, :], op=ALU.add), R=[('m0', par), ('m2', par)], W=['mrg'])
                    for sub in range(2):
                        for half in range(2):
                            b = 6 + half
                            for kc in range(8):
                                k.op('pe', lambda e: e.matmul(ps[:, b, :], lhsT=mrg[:, kc, sub * 128:(sub + 1) * 128], rhs=wo[:, kc, half * 512:(half + 1) * 512],
                                                              start=(kc == 0), stop=(kc == 7)), R=['mrg'], W=[('ps', b)])
                            k.op('dve', lambda e: e.scalar_tensor_tensor(out=xr[:, sub, half * 512:(half + 1) * 512], in0=xr[:, sub, half * 512:(half + 1) * 512],
                                                                         scalar=ALPHA, in1=ps[:, b, :], op0=ALU.mult, op1=ALU.add), R=['xr', ('ps', b)], W=['xr'])
                        lnorm(xr, 'xr', sub, 0, lng, 'lng')
                    tposes(xr, 2, x1T, 'xr', 'x1T')
                    for h in range(4):
                        b = 4 + (h % 2)
                        for kc in range(8):
                            k.op('pe', lambda e: e.matmul(ps[:, b, 0:256], lhsT=wq[:, kc, h * 128:(h + 1) * 128], rhs=x1T[:, kc, :], start=(kc == 0), stop=(kc == 7)),
                                 R=['x1T'], W=[('ps', b)])
                        k.op('act', lambda e: e.activation(out=qcT[:, h, :], in_=ps[:, b, 0:256], func=AF.Copy), R=[('ps', b)], W=[('qcT', h)])
                    for h in range(4):
                        for mc in range(2):
                            b = (h * 2 + mc) % 4
                            k.op('pe', lambda e: e.matmul(ps[:, b, 0:256], lhsT=kcT[:, seg, h, mc * 128:(mc + 1) * 128], rhs=qcT[:, h, :], start=True, stop=True),
                                 R=[('qcT', h)], W=[('ps', b)])
                            k.op('act', lambda e: e.activation(out=pT[:, h, mc, :], in_=ps[:, b, 0:256], func=AF.Exp, scale=128.0 ** -0.5), R=[('ps', b)], W=[('pT', h)])
                    for h in range(4):
                        bX, bY = 4 + (h % 2), 6 + (h % 2)
                        for mc in range(2):
                            k.op('pe', lambda e: e.matmul(ps[:, bX, 0:256], lhsT=vcs[:, seg, mc, h * 128:(h + 1) * 128], rhs=pT[:, h, mc, :], start=(mc == 0), stop=(mc == 1)),
                                 R=[('pT', h)], W=[('ps', bX)])
                        for mc in range(2):
                            k.op('pe', lambda e: e.matmul(ps[:, bY, 0:256], lhsT=ones[:], rhs=pT[:, h, mc, :], start=(mc == 0), stop=(mc == 1)),
                                 R=[('pT', h)], W=[('ps', bY)])
                        k.op('dve', lambda e: e.reciprocal(out=rdn[:, h, :], in_=ps[:, bY, 0:256]), R=[('ps', bY)], W=[('rdn', h)])
                        k.op('dve', lambda e: e.tensor_tensor(out=ocT[:, h, :], in0=ps[:, bX, 0:256], in1=rdn[:, h, :], op=ALU.mult), R=[('ps', bX), ('rdn', h)], W=['ocT'])
                    for sub in range(2):
                        for half in range(2):
                            b = 2 * (sub % 2) + half
                            for h in range(4):
                                k.op('pe', lambda e: e.matmul(ps[:, b, :], lhsT=ocT[:, h, sub * 128:(sub + 1) * 128], rhs=wco[:, h, half * 512:(half + 1) * 512],
                                                              start=(h == 0), stop=(h == 3)), R=['ocT'], W=[('ps', b)])
                            k.op('dve', lambda e: e.scalar_tensor_tensor(out=xr[:, sub, half * 512:(half + 1) * 512], in0=xr[:, sub, half * 512:(half + 1) * 512],
                                                                         scalar=ALPHA, in1=ps[:, b, :], op0=ALU.mult, op1=ALU.add), R=['xr', ('ps', b)], W=['xr'])
                        lnorm(xr, 'xr', sub, 2, lng, 'lng')
                    k.st(Dy(X2[tb:tb + 256, :].rearrange("(s p) f -> p s f", p=128), 256 * D), xr[:], R=['xr'])
                    tposes(xr, 2, x2T, 'xr', 'x2T', bank0=4)
                    k.st(Dy(X2T[:, :, PO + tb:PO + tb + 256].rearrange("k p t -> p k t"), 256), x2T[:], R=['x2T'])
                k.loop(NT2 // 2, body3b)

        if upto >= 5:
          with ExitStack() as p4c:
            psb = lambda n, sh, dt=F32: p4c.enter_context(nc.sbuf_tensor(f"{n}_sb{l}", sh, dt))
            wup = psb("fwup", [128, 8, 2 * DFF], BF16)
            xm = psb("fxm", [128, 8, 512], BF16); hal = psb("fhal", [128, 8, 2], BF16); fl = psb("ffl", [128, 4])
            cv = psb("fcv", [128, 2, 512]); ga = psb("fga", [128, 2, 512]); h_st = psb("fhst", [128, 24, 512], BF16)
            wload(k, wup, w_up[l], 8, 2 * DFF, stage=wstg)

            def body4a():
                k.ld(xm[:], Dy(X2T[:, :, PO:PO + 512].rearrange("k p t -> p k t"), 512), W=['xm'])
                k.ld(hal[:, :, 0:1], Dy(X2T[:, :, PO - 1:PO].rearrange("k p t -> p k t"), 512), W=['hal'], slow=True)
                k.ld(hal[:, :, 1:2], Dy(X2T[:, :, PO + 512:PO + 513].rearrange("k p t -> p k t"), 512), W=['hal'], slow=True)
                k.ld(fl[:], Dy(flg_d[0:1, :, :].rearrange("o p f -> p (o f)"), 512), W=['fl'])
                k.op('dve', lambda e: e.tensor_scalar(out=hal[:, :, 0:1], in0=hal[:, :, 0:1], scalar1=fl[:, 0:1], scalar2=None, op0=ALU.mult), R=['hal', 'fl'], W=['hal'])
                k.op('dve', lambda e: e.tensor_scalar(out=hal[:, :, 1:2], in0=hal[:, :, 1:2], scalar1=fl[:, 1:2], scalar2=None, op0=ALU.mult), R=['hal', 'fl'], W=['hal'])
                for c in range(24):
                    par = c % 2
                    bG, bU, bH = par, 2 + par, 6 + par
                    gs = slice(c * 128, (c + 1) * 128)
                    us = slice(DFF + c * 128, DFF + (c + 1) * 128)
                    for kc in range(8):
                        k.op('pe', lambda e: e.matmul(ps[:, bG, :], lhsT=wup[:, kc, gs], rhs=xm[:, kc, :], start=(kc == 0), stop=(kc == 7)), R=['xm'], W=[('ps', bG)])
                    for kc in range(8):
                        k.op('pe', lambda e: e.matmul(ps[:, bH, 0:2], lhsT=wup[:, kc, gs], rhs=hal[:, kc, :], start=(kc == 0), stop=(kc == 7)), R=['hal'], W=[('ps', bH)])
                    for kc in range(8):
                        k.op('pe', lambda e: e.matmul(ps[:, bU, :], lhsT=wup[:, kc, us], rhs=xm[:, kc, :], start=(kc == 0), stop=(kc == 7)), R=['xm'], W=[('ps', bU)])
                    w0, w1, w2, bb = pv[:, 88 + c:89 + c], pv[:, 112 + c:113 + c], pv[:, 136 + c:137 + c], pv[:, 160 + c:161 + c]
                    CV = ('cv', par)
                    k.op('dve', lambda e: e.tensor_scalar(out=cv[:, par, :], in0=ps[:, bG, :], scalar1=w1, scalar2=bb, op0=ALU.mult, op1=ALU.add), R=[('ps', bG), 'pv'], W=[CV])
                    k.op('dve', lambda e: e.scalar_tensor_tensor(out=cv[:, par, 1:512], in0=ps[:, bG, 0:511], scalar=w0, in1=cv[:, par, 1:512], op0=ALU.mult, op1=ALU.add), R=[('ps', bG), CV], W=[CV])
                    k.op('dve', lambda e: e.scalar_tensor_tensor(out=cv[:, par, 0:511], in0=ps[:, bG, 1:512], scalar=w2, in1=cv[:, par, 0:511], op0=ALU.mult, op1=ALU.add), R=[('ps', bG), CV], W=[CV])
                    k.op('dve', lambda e: e.scalar_tensor_tensor(out=cv[:, par, 0:1], in0=ps[:, bH, 0:1], scalar=w0, in1=cv[:, par, 0:1], op0=ALU.mult, op1=ALU.add), R=[('ps', bH), CV], W=[CV])
                    k.op('dve', lambda e: e.scalar_tensor_tensor(out=cv[:, par, 511:512], in0=ps[:, bH, 1:2], scalar=w2, in1=cv[:, par, 511:512], op0=ALU.mult, op1=ALU.add), R=[('ps', bH), CV], W=[CV])
                    k.op('act', lambda e: e.activation(out=ga[:, par, :], in_=cv[:, par, :], func=AF.Gelu_apprx_tanh), R=[CV], W=[('ga', par)])
                    k.op('dve', lambda e: e.tensor_tensor(out=h_st[:, c, :], in0=ps[:, bU, :], in1=ga[:, par, :], op=ALU.mult), R=[('ps', bU), ('ga', par)], W=[('h_st', c // 8)])
                for g3 in range(3):
                    k.st(Dy(HT[g3 * 8:(g3 + 1) * 8, :, 0:512].rearrange("k p t -> p k t"), 512), h_st[:, g3 * 8:(g3 + 1) * 8, :], R=[('h_st', g3)])
            k.loop(NT, body4a)

        if upto >= 5:
          with ExitStack() as p4c:
            psb = lambda n, sh, dt=F32: p4c.enter_context(nc.sbuf_tensor(f"{n}_sb{l}", sh, dt))
            wd = psb("fwd", [128, 24, D], BF16); ln3 = psb("fln3", [128, 2, D])
            hT = psb("fhT", [128, 24, 512], BF16); xr = psb("fxr", [128, 4, D]); xTs = psb("fxT", [128, 8, 512], BF16)
            stt = psb("fst", [128, 4, 12]); mv = psb("fmv", [128, 4, 2]); sd = psb("fsd", [128, 4, 1]); nmr = psb("fnmr", [128, 4, 1])
            wload(k, wd, w_dn[l], 24, D, stage=wstg)
            k.ld(ln3[:], bc_d[l, 4:6].rearrange("j p f -> p j f"), W=['ln3'])
            dst = y_d if last else XN

            def body4b():
                for g3 in range(3):
                    k.ld(hT[:, g3 * 8:(g3 + 1) * 8, :], Dy(HT[g3 * 8:(g3 + 1) * 8, :, 0:512].rearrange("k p t -> p k t"), 512), W=['hT'])
                k.ld(xr[:], Dy(X2[0:512, :].rearrange("(s p) f -> p s f", p=128), 512 * D), W=['xr'])
                for sub in range(4):
                    for half in range(2):
                        b = 4 + half
                        for kc in range(24):
                            k.op('pe', lambda e: e.matmul(ps[:, b, :], lhsT=hT[:, kc, sub * 128:(sub + 1) * 128], rhs=wd[:, kc, half * 512:(half + 1) * 512],
                                                          start=(kc == 0), stop=(kc == 23)), R=['hT'], W=[('ps', b)])
                        k.op('dve', lambda e: e.scalar_tensor_tensor(out=xr[:, sub, half * 512:(half + 1) * 512], in0=xr[:, sub, half * 512:(half + 1) * 512],
                                                                     scalar=ALPHA, in1=ps[:, b, :], op0=ALU.mult, op1=ALU.add), R=[('xr', sub), 'xr', ('ps', b)], W=[('xr', sub)])
                    XK = ('xr', sub)
                    k.op('dve', lambda e: e.bn_stats(out=stt[:, sub, 0:6], in_=xr[:, sub, 0:512]), R=[XK], W=[('stt', sub)])
                    k.op('dve', lambda e: e.bn_stats(out=stt[:, sub, 6:12], in_=xr[:, sub, 512:1024]), R=[XK], W=[('stt', sub)])
                    k.op('dve', lambda e: e.bn_aggr(out=mv[:, sub, :], in_=stt[:, sub, :]), R=[('stt', sub)], W=[('mv', sub)])
                    k.op('act', lambda e: e.activation(out=sd[:, sub, :], in_=mv[:, sub, 1:2], func=AF.Sqrt, bias=LN_EPS), R=[('mv', sub)], W=[('sd', sub)])
                    k.op('dve', lambda e: e.reciprocal(out=sd[:, sub, :], in_=sd[:, sub, :]), R=[('sd', sub)], W=[('sd', sub)])
                    k.op('dve', lambda e: e.tensor_scalar(out=nmr[:, sub, :], in0=mv[:, sub, 0:1], scalar1=sd[:, sub, :], scalar2=-1.0, op0=ALU.mult, op1=ALU.mult),
                         R=[('mv', sub), ('sd', sub)], W=[('nmr', sub)])
                    k.op('act', lambda e: e.activation(out=xr[:, sub, :], in_=xr[:, sub, :], func=AF.Identity, scale=sd[:, sub, :], bias=nmr[:, sub, :]),
                         R=[XK, ('sd', sub), ('nmr', sub)], W=[XK])
                    k.op('pool', lambda e: e.tensor_tensor(out=xr[:, sub, :], in0=xr[:, sub, :], in1=ln3[:, 0, :], op=ALU.mult), R=[XK, 'ln3'], W=[XK])
                    k.op('pool', lambda e: e.tensor_tensor(out=xr[:, sub, :], in0=xr[:, sub, :], in1=ln3[:, 1, :], op=ALU.add), R=[XK, 'ln3'], W=[XK])
                XALL = [('xr', s_) for s_ in range(4)]
                k.st(Dy(dst[0:512, :].rearrange("(s p) f -> p s f", p=128), 512 * D), xr[:], R=XALL)
                if not last:
                    for kc in range(8):
                        b = kc % 4
                        for sub in range(4):
                            k.op('pe', lambda e: e.transpose(out=ps[:, b, sub * 128:(sub + 1) * 128], in_=xr[:, sub, kc * 128:(kc + 1) * 128], identity=ident[:]),
                                 R=XALL + ['c'], W=[('ps', b)])
                        if kc % 2 == 0:
                            k.op('act', lambda e: e.activation(out=xTs[:, kc, :], in_=ps[:, b, :], func=AF.Copy), R=[('ps', b)], W=['xTs'])
                        else:
                            k.op('dve', lambda e: e.tensor_copy(out=xTs[:, kc, :], in_=ps[:, b, :]), R=[('ps', b)], W=['xTs'])
                    k.st(Dy(XT[:, :, PO:PO + 512].rearrange("k p t -> p k t"), 512), xTs[:], R=['xTs'])
            k.loop(NT, body4b)

    k.sync_all()
    es.close()
    return nc


SEG_FULL = 8192
N_LAYERS = 4


def kernel(**inp):
    inp = {k_: np.asarray(v) for k_, v in inp.items()}
    SEG, L = SEG_FULL, N_LAYERS
    T = 2 * SEG
    nc = build(SEG, L, debug=False, upto=5, dense=True)
    pvv, bcv = pack_small(inp, L)
    ident, R, SHm, BOm, Mw = const_tables()
    w_in_p = np.ascontiguousarray(inp['w_in'][:, :, w_in_perm_index()])
    shared = {"w_in": w_in_p, "w_rg": inp['w_rec_gate'], "w_ig": inp['w_in_gate'],
              "w_br_a": inp['w_br_a'], "w_br_b": inp['w_br_b'], "w_br_c": inp['w_br_c'], "w_out": inp['w_out'],
              "w_cq": inp['w_cq'], "w_ckv": inp['w_ckv'], "w_co": inp['w_co'], "w_up": inp['w_up'],
              "w_down": inp['w_down'], "pv": pvv, "bc": bcv, "ident": ident, "Rm": R, "SH": SHm, "BO": BOm, "Mw": Mw}
    tabs = {j: host_tables(SEG, j) for j in (True, False)}

    def cmap(x, mem2, joined):
        C, S, flg, mb = tabs[joined]
        m = dict(shared)
        m.update({"x": np.ascontiguousarray(x.reshape(T, D)), "mem": np.ascontiguousarray(mem2),
                  "ropeC": C, "ropeS": S, "flg": flg, "mb": mb})
        return m
    maps = []
    for c in range(2):
        maps.append(cmap(inp['x_prompt'][c], np.stack([inp['mem_prompt'][c]] * 2), True))
    for c in range(4):
        maps.append(cmap(inp['x_sample'][2 * c:2 * c + 2], inp['mem_sample'][2 * c:2 * c + 2], False))
    maps += [maps[-1], maps[-1]]
    res = run_bass_kernel_spmd(nc, maps, core_ids=list(range(8)))
    y_prompt = np.stack([np.asarray(res.results[c]["y"], dtype=np.float32) for c in range(2)])
    ys = [np.asarray(res.results[2 + c]["y"], dtype=np.float32).reshape(2, SEG, D) for c in range(4)]
    y_sample = np.concatenate(ys, axis=0)
    return (y_prompt, y_sample)
```
